# Optimizing a Trainium2 kernel written in Bass

```python
import math
import jax, jax.numpy as jnp
from jax import lax
import numpy as np

D_MODEL = 2048
BATCH = 2
SEQ = 16384
DEPTH = 1

N_META = 16
CHUNK = 128
PAD = CHUNK - N_META
A_HEADS = 8
A_DK = 64
A_DV = 2 * A_DK
Q_W = A_HEADS * 2 * A_DK
V_W = A_HEADS * A_DV
D_SSM = D_MODEL
S_HEADDIM = 64
S_HEADS = D_SSM // S_HEADDIM
S_GROUPS = 4
S_HPG = S_HEADS // S_GROUPS
S_STATE = 128
S_CONV = 4
CONV_DIM = D_SSM + 2 * S_GROUPS * S_STATE
SPLIT_SIZES = [Q_W, Q_W, V_W, V_W, D_SSM, CONV_DIM, S_HEADS, D_MODEL, D_MODEL]
P_IN = sum(SPLIT_SIZES)
SPLIT_IDX = [int(v) for v in np.cumsum(SPLIT_SIZES)[:-1]]
EPS = 1e-6
NEG = -1e30

kernel_name = 'hybrid_diffattn_ssd_gated_merge'


def rmsnorm(x, g):
    xf = x.astype(jnp.float32)
    y = xf * lax.rsqrt(jnp.mean(xf * xf, axis=-1, keepdims=True) + EPS)
    return (y * g.astype(jnp.float32)).astype(x.dtype)


def pad_front(a, n):
    cfg = [(0, 0)] * a.ndim
    cfg[1] = (n, 0)
    return jnp.pad(a, cfg)


def diff_attention(q, k, v, lam):
    b, lp = q.shape[0], q.shape[1]
    nblk = lp // CHUNK
    slopes = 2.0 ** (-8.0 * jnp.arange(1, A_HEADS + 1, dtype=jnp.float32) / A_HEADS)
    kpos = jnp.arange(lp)
    kf = k.astype(jnp.float32)
    vf = v.astype(jnp.float32)
    scale = A_DK ** -0.5

    def block(i):
        q0 = i * CHUNK
        qb = lax.dynamic_slice_in_dim(q, q0, CHUNK, axis=1).astype(jnp.float32) * scale
        s = jnp.einsum('bqhmd,bkhmd->bhmqk', qb, kf)
        qpos = q0 + jnp.arange(CHUNK)
        dist = jnp.abs(qpos[:, None] - kpos[None, :]).astype(jnp.float32)
        s = s - (slopes[:, None, None] * dist)[None, :, None]
        valid = (kpos[None, :] <= qpos[:, None]) & (kpos[None, :] >= PAD)
        s = jnp.where(valid, s, NEG)
        p = jax.nn.softmax(s, axis=-1)
        a = p[:, :, 0] - lam * p[:, :, 1]
        return jnp.einsum('bhqk,bkhe->bqhe', a, vf)

    o = lax.map(block, jnp.arange(nblk))
    return o.transpose(1, 0, 2, 3, 4).reshape(b, lp, A_HEADS, A_DV)


def ssd_chunked(xs, dt, a, bm, cm):
    b, lp = xs.shape[0], xs.shape[1]
    c = lp // CHUNK
    xdt = (xs * dt[..., None]).reshape(b, c, CHUNK, S_GROUPS, S_HPG, S_HEADDIM)
    da = (dt.reshape(b, c, CHUNK, S_GROUPS, S_HPG) * a.reshape(S_GROUPS, S_HPG)).transpose(0, 1, 3, 4, 2)
    acs = jnp.cumsum(da, axis=-1)
    bb = bm.reshape(b, c, CHUNK, S_GROUPS, S_STATE)
    cc = cm.reshape(b, c, CHUNK, S_GROUPS, S_STATE)
    tril = jnp.tril(jnp.ones((CHUNK, CHUNK), dtype=bool))
    lmat = jnp.exp(jnp.where(tril, acs[..., :, None] - acs[..., None, :], -jnp.inf))
    cb = jnp.einsum('bclgn,bcsgn->bcgls', cc, bb)
    y_diag = jnp.einsum('bcgls,bcghls,bcsghp->bclghp', cb, lmat, xdt)
    decay_states = jnp.exp(acs[..., -1:] - acs)
    states = jnp.einsum('bclgn,bcghl,bclghp->bcghpn', bb, decay_states, xdt)
    chunk_decay = jnp.exp(acs[..., -1])

    def step(s, inp):
        st, dec = inp
        return dec[..., None, None] * s + st, s

    init = jnp.zeros((b, S_GROUPS, S_HPG, S_HEADDIM, S_STATE), jnp.float32)
    _, prev = lax.scan(step, init, (states.transpose(1, 0, 2, 3, 4, 5), chunk_decay.transpose(1, 0, 2, 3)))
    prev = prev.transpose(1, 0, 2, 3, 4, 5)
    y_off = jnp.einsum('bclgn,bcghpn,bcghl->bclghp', cc, prev, jnp.exp(acs))
    return (y_diag + y_off).reshape(b, lp, S_HEADS, S_HEADDIM)


def setup_inputs(seed: int = 0) -> dict:
    key = jax.random.key(seed)
    ks = jax.random.split(key, 20)
    f32 = jnp.float32
    nrm = lambda k, s: jax.random.normal(k, s, f32)
    x = nrm(ks[0], (BATCH, SEQ, D_MODEL))
    meta = nrm(ks[1], (N_META, D_MODEL))
    norm_g = 1.0 + 0.02 * nrm(ks[2], (DEPTH, D_MODEL))
    w_in = nrm(ks[3], (DEPTH, D_MODEL, P_IN)) * D_MODEL ** -0.5
    conv_w = nrm(ks[4], (DEPTH, S_CONV, CONV_DIM)) * S_CONV ** -0.5
    conv_b = 0.02 * nrm(ks[5], (DEPTH, CONV_DIM))
    u = jax.random.uniform(ks[6], (DEPTH, S_HEADS), f32)
    dt0 = jnp.exp(u * (math.log(0.1) - math.log(0.001)) + math.log(0.001))
    dt_bias = dt0 + jnp.log(-jnp.expm1(-dt0))
    a_log = jnp.log(jax.random.uniform(ks[7], (DEPTH, S_HEADS), f32, minval=1.0, maxval=16.0))
    d_skip = 1.0 + 0.02 * nrm(ks[8], (DEPTH, S_HEADS))
    ssm_norm_g = 1.0 + 0.02 * nrm(ks[9], (DEPTH, D_SSM))
    lam_q1 = 0.1 * nrm(ks[10], (DEPTH, A_DK))
    lam_k1 = 0.1 * nrm(ks[11], (DEPTH, A_DK))
    lam_q2 = 0.1 * nrm(ks[12], (DEPTH, A_DK))
    lam_k2 = 0.1 * nrm(ks[13], (DEPTH, A_DK))
    subln_g = 1.0 + 0.02 * nrm(ks[14], (DEPTH, A_DV))
    w_br_attn = nrm(ks[15], (DEPTH, V_W, D_MODEL)) * V_W ** -0.5
    w_br_ssm = nrm(ks[16], (DEPTH, D_SSM, D_MODEL)) * D_SSM ** -0.5
    w_out = nrm(ks[17], (DEPTH, D_MODEL, D_MODEL)) * D_MODEL ** -0.5
    final_g = 1.0 + 0.02 * nrm(ks[18], (D_MODEL,))
    return {'x': x, 'meta': meta, 'norm_g': norm_g, 'w_in': w_in, 'conv_w': conv_w,
            'conv_b': conv_b, 'dt_bias': dt_bias, 'a_log': a_log, 'd_skip': d_skip,
            'ssm_norm_g': ssm_norm_g, 'lam_q1': lam_q1, 'lam_k1': lam_k1,
            'lam_q2': lam_q2, 'lam_k2': lam_k2, 'subln_g': subln_g,
            'w_br_attn': w_br_attn, 'w_br_ssm': w_br_ssm, 'w_out': w_out, 'final_g': final_g}


def reference(x, meta, norm_g, w_in, conv_w, conv_b, dt_bias, a_log, d_skip, ssm_norm_g,
              lam_q1, lam_k1, lam_q2, lam_k2, subln_g, w_br_attn, w_br_ssm, w_out, final_g):
    b, s, d = x.shape
    h = jnp.concatenate([jnp.broadcast_to(meta[None].astype(x.dtype), (b, N_META, d)), x], axis=1)
    L = s + N_META
    for l in range(DEPTH):
        u = rmsnorm(h, norm_g[l])
        zp = u @ w_in[l]
        q, k, v, g_att, z_ssm, xbc, dt, m_att, m_ssm = jnp.split(zp, SPLIT_IDX, axis=-1)

        lam_init = 0.8 - 0.6 * math.exp(-0.3 * l)
        lq1, lk1 = lam_q1[l].astype(jnp.float32), lam_k1[l].astype(jnp.float32)
        lq2, lk2 = lam_q2[l].astype(jnp.float32), lam_k2[l].astype(jnp.float32)
        lam = jnp.exp(jnp.sum(lq1 * lk1)) - jnp.exp(jnp.sum(lq2 * lk2)) + lam_init
        qa = pad_front(q.reshape(b, L, A_HEADS, 2, A_DK), PAD)
        ka = pad_front(k.reshape(b, L, A_HEADS, 2, A_DK), PAD)
        va = pad_front(v.reshape(b, L, A_HEADS, A_DV), PAD)
        oa = diff_attention(qa, ka, va, lam)[:, PAD:]
        oa = rmsnorm(oa, subln_g[l]) * (1.0 - lam_init)
        oa = oa.reshape(b, L, V_W).astype(x.dtype) * jax.nn.silu(g_att)
        y_att = oa @ w_br_attn[l]

        xbc = lax.conv_general_dilated(xbc, conv_w[l][:, None, :].astype(xbc.dtype), (1,),
                                       [(S_CONV - 1, 0)], dimension_numbers=('NWC', 'WIO', 'NWC'),
                                       feature_group_count=CONV_DIM)
        xbc = jax.nn.silu(xbc + conv_b[l].astype(xbc.dtype))
        xs, bm, cm = jnp.split(xbc.astype(jnp.float32), [D_SSM, D_SSM + S_GROUPS * S_STATE], axis=-1)
        xs = xs.reshape(b, L, S_HEADS, S_HEADDIM)
        bm = bm.reshape(b, L, S_GROUPS, S_STATE)
        cm = cm.reshape(b, L, S_GROUPS, S_STATE)
        dtf = jax.nn.softplus(dt.astype(jnp.float32) + dt_bias[l].astype(jnp.float32))
        a = -jnp.exp(a_log[l].astype(jnp.float32))
        ys = ssd_chunked(pad_front(xs, PAD), pad_front(dtf, PAD), a,
                         pad_front(bm, PAD), pad_front(cm, PAD))[:, PAD:]
        ys = ys + xs * d_skip[l].astype(jnp.float32)[None, None, :, None]
        yg = ys.reshape(b, L, D_SSM) * jax.nn.silu(z_ssm.astype(jnp.float32))
        yg = rmsnorm(yg.reshape(b, L, S_GROUPS, D_SSM // S_GROUPS),
                     ssm_norm_g[l].reshape(S_GROUPS, D_SSM // S_GROUPS))
        y_ssm = yg.reshape(b, L, D_SSM).astype(x.dtype) @ w_br_ssm[l]

        merged = jax.nn.sigmoid(m_att) * y_att + jax.nn.sigmoid(m_ssm) * y_ssm
        h = h + merged @ w_out[l]
    return rmsnorm(h[:, N_META:], final_g)
```

```python
import math
from contextlib import ExitStack

import numpy as np
import ml_dtypes

import concourse.bass as bass
import concourse.mybir as mybir
from concourse.bass_utils import run_bass_kernel_spmd

F32 = mybir.dt.float32
BF16 = mybir.dt.bfloat16
I32 = mybir.dt.int32
U8 = mybir.dt.uint8
AF = mybir.ActivationFunctionType
ALU = mybir.AluOpType
AX = mybir.AxisListType

D = 2048
KC = 16
N_META = 16
PAD = 112
EPS = 1e-6
NEGBIG = -30000.0
LAM_INIT = 0.8 - 0.6 * math.exp(0.0)

COMPUTE = ("pe", "act", "dve", "pool")
NDMA_SEM = 6


class Prog:
    def __init__(self, nc, same_engine_sync=("act", "dve", "pool")):
        self.nc = nc
        self.ops = []
        self.same_sync = set(same_engine_sync)
        self.last_writer = {}
        self.readers = {}
        self.eng_ops = {e: [] for e in COMPUTE + ("sp",)}
        self.dma_count = {}
        self.cc_count = 0
        self.pending_barrier = {}

    def op(self, eng, fn, reads=(), writes=(), dma=False, cc=False):
        idx = len(self.ops)
        o = dict(idx=idx, eng=eng, fn=fn, dma=dma, cc=cc, deps=set(), signal=False)
        for k in reads:
            w = self.last_writer.get(k)
            if w is not None:
                o["deps"].add(w)
        for k in writes:
            w = self.last_writer.get(k)
            if w is not None:
                o["deps"].add(w)
            for r in self.readers.get(k, ()):
                o["deps"].add(r)
        for k in writes:
            self.last_writer[k] = idx
            self.readers[k] = []
        for k in reads:
            if k in writes:
                continue
            self.readers.setdefault(k, []).append(idx)
        if dma:
            n = self.dma_count.get(eng, 0)
            o["dma_n"] = n
            self.dma_count[eng] = n + 1
        if cc:
            o["cc_n"] = self.cc_count
            self.cc_count += 1
        pb = self.pending_barrier.pop(eng, None)
        if pb:
            o["deps"].update(pb)
        o["deps"].discard(idx)
        self.ops.append(o)
        self.eng_ops[eng].append(idx)
        return idx

    def barrier(self):
        lasts = set()
        for e, lst in self.eng_ops.items():
            if lst:
                lasts.add(lst[-1])
            dm = [i for i in lst if self.ops[i]["dma"]]
            lasts.update(dm[-NDMA_SEM:])
            lasts.update(i for i in lst if self.ops[i]["cc"])
        for e in self.eng_ops:
            self.pending_barrier.setdefault(e, set()).update(lasts)

    def emit(self, stack):
        nc = self.nc
        ops = self.ops
        for o in ops:
            best = {}
            nd = set()
            for d in o["deps"]:
                do = ops[d]
                if do["dma"] or do["cc"]:
                    nd.add(d)
                elif best.get(do["eng"], -1) < d:
                    best[do["eng"]] = d
            nd.update(best.values())
            o["deps"] = nd
        for o in ops:
            for d in o["deps"]:
                do = ops[d]
                if do["dma"] or do["cc"]:
                    continue
                if do["eng"] == o["eng"] and do["eng"] not in self.same_sync:
                    continue
                do["signal"] = True
        sems = {e: stack.enter_context(nc.semaphore("s_" + e)) for e in self.eng_ops}
        dsems = {e: [stack.enter_context(nc.semaphore("d_%s%d" % (e, i))) for i in range(NDMA_SEM)]
                 for e in self.dma_count}
        ccsem = stack.enter_context(nc.semaphore("ccsem"))
        cnt = {e: 0 for e in self.eng_ops}
        for o in ops:
            if o["dma"]:
                n = o["dma_n"]
                o["ev"] = (("d", o["eng"], n % NDMA_SEM), 16 * (n // NDMA_SEM + 1))
            elif o["cc"]:
                o["ev"] = (("c",), o["cc_n"] + 1)
            elif o["signal"]:
                cnt[o["eng"]] += 1
                o["ev"] = (("e", o["eng"]), cnt[o["eng"]])
            else:
                o["ev"] = None

        def semof(key):
            if key[0] == "e":
                return sems[key[1]]
            if key[0] == "c":
                return ccsem
            return dsems[key[1]][key[2]]

        block = stack.enter_context(nc.Block())

        def make(e):
            def body(engobj):
                waited = {}
                for i in self.eng_ops[e]:
                    o = ops[i]
                    need = {}
                    for d in o["deps"]:
                        do = ops[d]
                        if (not do["dma"]) and (not do["cc"]) and do["eng"] == e \
                                and e not in self.same_sync:
                            continue
                        k, v = do["ev"]
                        if need.get(k, 0) < v:
                            need[k] = v
                    if o["dma"]:
                        n = o["dma_n"]
                        if n >= NDMA_SEM:
                            k = ("d", e, n % NDMA_SEM)
                            v = 16 * (n // NDMA_SEM)
                            if need.get(k, 0) < v:
                                need[k] = v
                    if o["cc"] and o["cc_n"] > 0:
                        need[("c",)] = max(need.get(("c",), 0), o["cc_n"])
                    for k, v in need.items():
                        if waited.get(k, 0) >= v:
                            continue
                        engobj.wait_ge(semof(k), v)
                        waited[k] = v
                    ins = o["fn"](engobj)
                    if o["dma"]:
                        ins.then_inc(dsems[e][o["dma_n"] % NDMA_SEM], 16)
                    elif o["cc"]:
                        ins.then_inc(ccsem, 1)
                    elif o["signal"]:
                        ins.then_inc(sems[e], 1)
                if e in self.dma_count:
                    n = self.dma_count[e]
                    for s in range(NDMA_SEM):
                        c = len([j for j in range(n) if j % NDMA_SEM == s])
                        if c and waited.get(("d", e, s), 0) < 16 * c:
                            engobj.wait_ge(dsems[e][s], 16 * c)
                if e == "pool" and self.cc_count:
                    engobj.wait_ge(ccsem, self.cc_count)
            return body

        for e, deco in (("sp", block.sync), ("pe", block.tensor), ("act", block.scalar),
                        ("dve", block.vector), ("pool", block.gpsimd)):
            if self.eng_ops[e]:
                deco(make(e))


class Arena:
    def __init__(self, big, size):
        self.big = big
        self.size = size
        self.off = 0
        self.peak = 0

    def mark(self):
        return self.off

    def reset(self, m):
        self.off = m

    def alloc(self, shape, dt, parts=128):
        esz = {F32: 4, BF16: 2, I32: 4}[dt]
        n = int(np.prod(shape[1:])) * esz
        n_al = (n + 63) // 64 * 64
        assert self.off + n_al <= self.size, ("SBUF arena overflow", self.off, n_al, self.size)
        ap = self.big[0:shape[0], self.off:self.off + n].bitcast(dt)
        self.off += n_al
        self.peak = max(self.peak, self.off)
        if len(shape) == 3:
            ap = ap.rearrange("p (a b) -> p a b", a=shape[1])
        elif len(shape) == 4:
            ap = ap.rearrange("p (a b c) -> p a b c", a=shape[1], b=shape[2])
        return ap


def build_program(NCH, STOP=99, CC=True):
    L = NCH * 128
    TOK = (NCH - 1) * 128
    TQ = TOK // 4
    T3 = min(512, TQ)
    NT3 = T3 // 128
    NTB = TQ // T3
    NQB = (NCH - 1) // 4
    NB1 = NCH // 3
    assert NCH % 3 == 0 and (NCH - 1) % 4 == 0 and TQ % 128 == 0 and TQ % T3 == 0
    W1C = 2312
    PS = min(TOK, 1024)
    NP = TOK // PS
    CPP = PS // T3

    nc = bass.Bass("TRN2", target_bir_lowering=False)

    def din(name, shape, dt=F32):
        return nc.dram_tensor(name, list(shape), dt, kind="ExternalInput").ap()

    xp = din("xp", [L, D])
    w1 = din("w1", [D, W1C])
    wm = din("wm", [D, 4096])
    wa = din("wa", [1024, D])
    ws = din("ws", [D, D])
    wo = din("wo", [D, D])
    normg = din("normg", [128, KC])
    ssmg = din("ssmg", [128, KC])
    convw = din("convw", [128, 6, 4])
    convb = din("convb", [128, 6])
    dtb_i = din("dtb", [128, 8])
    alog_i = din("alog", [128, 8])
    dskip_i = din("dskip", [128, 8])
    subln_i = din("subln", [128, 128])
    finalg_i = din("finalg", [128, D])
    lamv_i = din("lamv", [128, 4, 64])
    identb_i = din("identb", [128, 128], BF16)
    identf_i = din("identf", [128, 128])
    tri_i = din("tri", [128, 128], BF16)
    umat_i = din("umat", [128, 128])
    mneg_i = din("mneg", [128, 128])
    btab_i = din("btab", [128, 2, 132])
    btab0_i = din("btab0", [128, 2, NQB])
    qaug_i = din("qaug", [2, 2, L], BF16)
    kaug_i = din("kaug", [2, L], BF16)
    idxx_i = din("idxx", [128, TQ // 128], I32)
    idxs_i = din("idxs", [128, 4 * 4 * NTB], I32)
    idxa_i = din("idxa", [128, 4 * 2 * NTB], I32)
    out = nc.dram_tensor("out", [TQ, D], F32, kind="ExternalOutput").ap()

    Qd = nc.dram_tensor("Qd", [2, 2, 66, L], BF16).ap()
    Kd = nc.dram_tensor("Kd", [2, 2, 66, L], BF16).ap()
    Vd = nc.dram_tensor("Vd", [2, NCH, 128, 144], BF16).ap()
    Gd = nc.dram_tensor("Gd", [L, 256], F32).ap()
    W3d = nc.dram_tensor("W3d", [16, 128, 56, 128], BF16).ap()
    Wod = nc.dram_tensor("Wod", [128, KC, D], BF16).ap()
    cin_s = nc.dram_tensor("cin_s", [NP, 512, PS], BF16)
    cout_s = nc.dram_tensor("cout_s", [NP, 2048, PS], BF16)
    cin_a = nc.dram_tensor("cin_a", [NP, 256, PS], BF16)
    cout_a = nc.dram_tensor("cout_a", [NP, 1024, PS], BF16)
    RG = [[0, 1, 2, 3], [4, 5, 6, 7]]

    with ExitStack() as st:
        SB_BYTES = 200 * 1024
        big = st.enter_context(nc.sbuf_tensor("big", [128, SB_BYTES], U8))
        psum = st.enter_context(nc.psum_tensor("psum", [128, 16384], U8))
        A = Arena(big, SB_BYTES)
        P = Prog(nc)

        def psf(bank, n=512, parts=128, off=0):
            b0 = bank * 2048 + off * 4
            return psum[0:parts, b0:b0 + n * 4].bitcast(F32)

        def psb(bank, n=1024, parts=128, off=0):
            b0 = bank * 2048 + off * 2
            return psum[0:parts, b0:b0 + n * 2].bitcast(BF16)

        def dma(eng, out_, in_, reads=(), writes=()):
            return P.op(eng, lambda e: e.dma_start(out=out_, in_=in_), reads=reads, writes=writes,
                        dma=True)

        identb = A.alloc([128, 128], BF16)
        identf = A.alloc([128, 128], F32)
        tri = A.alloc([128, 128], BF16)
        umat = A.alloc([128, 128], F32)
        mneg = A.alloc([128, 128], F32)
        onesf = A.alloc([128, 128], F32)
        normg_s = A.alloc([128, KC], F32)
        ssmg_s = A.alloc([128, KC], F32)
        convw_s = A.alloc([128, 6, 4], F32)
        convb_s = A.alloc([128, 6], F32)
        dtb_s = A.alloc([128, 8], F32)
        nega = A.alloc([128, 8], F32)
        dsk8 = A.alloc([128, 8], F32)
        dskB = A.alloc([128, 8, 64], F32)
        subln_s = A.alloc([128, 128], F32)
        lamv = A.alloc([128, 4, 64], F32)
        lamt = A.alloc([128, 2, 64], F32)
        lam2 = A.alloc([128, 2], F32)
        neglam = A.alloc([128, 1], F32)
        btab = A.alloc([128, 2, 132], F32)
        btab0 = A.alloc([128, 2, NQB], F32)
        idxx = A.alloc([128, TQ // 128], I32)
        idxs = A.alloc([128, 4 * 4 * NTB], I32)
        idxa = A.alloc([128, 4 * 2 * NTB], I32)

        for t, src, nm in ((identb, identb_i, "identb"), (identf, identf_i, "identf"),
                           (tri, tri_i, "tri"), (umat, umat_i, "umat"), (mneg, mneg_i, "mneg"),
                           (normg_s, normg, "normg"), (ssmg_s, ssmg, "ssmg"),
                           (convw_s, convw, "convw"), (convb_s, convb, "convb"),
                           (dtb_s, dtb_i, "dtb"), (nega, alog_i, "nega"), (dsk8, dskip_i, "dsk8"),
                           (subln_s, subln_i, "subln"), (lamv, lamv_i, "lamv"),
                           (btab, btab_i, "btab"), (btab0, btab0_i, "btab0"),
                           (idxx, idxx_i, "idxx"), (idxs, idxs_i, "idxs"), (idxa, idxa_i, "idxa")):
            dma("sp", t, src, writes=[nm])
        P.op("dve", lambda e: e.memset(onesf, 1.0), writes=["onesf"])
        P.op("act", lambda e: e.activation(out=nega, in_=nega, func=AF.Exp), reads=["nega"],
             writes=["nega"])
        P.op("dve", lambda e: e.tensor_scalar(out=nega, in0=nega, scalar1=-1.0, scalar2=None,
                                              op0=ALU.mult), reads=["nega"], writes=["nega"])
        P.op("dve", lambda e: e.tensor_copy(out=dskB, in_=dsk8.unsqueeze(2).broadcast_to([128, 8, 64])),
             reads=["dsk8"], writes=["dskB"])
        P.op("dve", lambda e: e.tensor_scalar(out=subln_s, in0=subln_s, scalar1=1.0 - LAM_INIT,
                                              scalar2=None, op0=ALU.mult), reads=["subln"],
             writes=["subln"])
        P.op("dve", lambda e: e.tensor_tensor(out=lamt[:, 0, :], in0=lamv[:, 0, :], in1=lamv[:, 1, :],
                                              op=ALU.mult), reads=["lamv"], writes=["lamt0"])
        P.op("dve", lambda e: e.tensor_tensor(out=lamt[:, 1, :], in0=lamv[:, 2, :], in1=lamv[:, 3, :],
                                              op=ALU.mult), reads=["lamv"], writes=["lamt1"])
        P.op("dve", lambda e: e.reduce_sum(out=lam2, in_=lamt, axis=AX.X), reads=["lamt0", "lamt1"],
             writes=["lam2"])
        P.op("act", lambda e: e.activation(out=lam2, in_=lam2, func=AF.Exp), reads=["lam2"],
             writes=["lam2"])
        P.op("dve", lambda e: e.scalar_tensor_tensor(out=neglam, in0=lam2[:, 1:2], scalar=-LAM_INIT,
                                                     in1=lam2[:, 0:1], op0=ALU.add, op1=ALU.subtract),
             reads=["lam2"], writes=["neglam"])

        for hl in range(2):
            for m in range(2):
                dma("sp", Qd[hl, m, 64:66, :], qaug_i[hl], writes=[("Qd", hl, m)])
                dma("sp", Kd[hl, m, 64:66, :], kaug_i, writes=[("Kd", hl, m)])

        m_persist = A.mark()

        W1sb = A.alloc([128, KC, W1C], BF16)
        m_p1 = A.mark()
        stage = [A.alloc([128, 4096], F32) for _ in range(2)]
        stb = [A.alloc([128, 4096], BF16) for _ in range(2)]
        nstage = [0]

        def cast_rows(src_ap, ncols, scale_ap, dst_fn, eng_c):
            i = nstage[0]
            nstage[0] += 1
            s = i % 2
            dma("sp", stage[s][:, 0:ncols], src_ap, writes=[("stage", s)])
            if scale_ap is not None:
                P.op(eng_c, lambda e: e.tensor_scalar(out=stb[s][:, 0:ncols], in0=stage[s][:, 0:ncols],
                                                      scalar1=scale_ap, scalar2=None, op0=ALU.mult),
                     reads=[("stage", s), "normg", "ssmg"], writes=[("stb", s)])
            else:
                P.op(eng_c, lambda e: e.tensor_copy(out=stb[s][:, 0:ncols], in_=stage[s][:, 0:ncols]),
                     reads=[("stage", s)], writes=[("stb", s)])
            dst_fn(stb[s][:, 0:ncols], ("stb", s))

        for kc in range(KC):
            def dst(ap, key, kc=kc):
                P.op("act", lambda e: e.copy(out=W1sb[:, kc, :], in_=ap), reads=[key],
                     writes=[("W1", kc)])
            cast_rows(w1[kc * 128:(kc + 1) * 128, :], W1C, normg_s[:, kc:kc + 1], dst,
                      "dve" if kc % 2 == 0 else "pool")
        for kc in range(KC):
            def dst(ap, key, kc=kc):
                for half in range(2):
                    dma("sp", W3d[:, :, half * 16 + kc, :].rearrange("j p c -> p j c"),
                        ap[:, half * 2048:(half + 1) * 2048].rearrange("p (j c) -> p j c", c=128),
                        reads=[key], writes=["W3d"])
            cast_rows(wm[kc * 128:(kc + 1) * 128, :], 4096, normg_s[:, kc:kc + 1], dst,
                      "dve" if kc % 2 == 0 else "pool")
        for kc in range(8):
            def dst(ap, key, kc=kc):
                dma("sp", W3d[:, :, 32 + kc, :].rearrange("j p c -> p j c"),
                    ap.rearrange("p (j c) -> p j c", c=128), reads=[key], writes=["W3d"])
            cast_rows(wa[kc * 128:(kc + 1) * 128, :], D, None, dst, "dve" if kc % 2 == 0 else "pool")
        for kc in range(KC):
            def dst(ap, key, kc=kc):
                dma("sp", W3d[:, :, 40 + kc, :].rearrange("j p c -> p j c"),
                    ap.rearrange("p (j c) -> p j c", c=128), reads=[key], writes=["W3d"])
            cast_rows(ws[kc * 128:(kc + 1) * 128, :], D, ssmg_s[:, kc:kc + 1], dst,
                      "dve" if kc % 2 == 0 else "pool")
        for kc in range(KC):
            def dst(ap, key, kc=kc):
                dma("sp", Wod[:, kc, :], ap, reads=[key], writes=["Wod"])
            cast_rows(wo[kc * 128:(kc + 1) * 128, :], D, None, dst, "dve" if kc % 2 == 0 else "pool")
        P.barrier()
        if STOP == 0:
            P.emit(st)
            return nc
        A.reset(m_p1)

        xt = [A.alloc([128, D], F32) for _ in range(2)]
        xn = [A.alloc([128, D], BF16) for _ in range(2)]
        uT = A.alloc([128, KC, 384], BF16)
        ss3 = A.alloc([128, 4], F32)
        rs3 = A.alloc([128, 4], F32)
        qkst = A.alloc([128, 8, 384], BF16)
        vst = A.alloc([128, 3, 2, 144], BF16)
        gst = A.alloc([128, 3, 256], F32)
        zs = A.alloc([128, 3, 512], F32)
        dtr = A.alloc([128, 3, 8], F32)
        dtt = A.alloc([128, 3, 8], F32)
        dte = A.alloc([128, 3, 8], F32)
        dtv = A.alloc([128, 3, 8], F32)
        xraw = A.alloc([128, 6, 387], F32)
        accb = [A.alloc([128, 384], F32) for _ in range(2)]
        xc = A.alloc([128, 6, 384], F32)
        ctmp = A.alloc([128, 384], F32)
        xs_tm = A.alloc([128, 8, 64], F32)
        B_bf = A.alloc([128, 128], BF16)
        BT_bf = A.alloc([128, 128], BF16)
        CT_bf = A.alloc([128, 128], BF16)
        da = A.alloc([128, 8], F32)
        daU = A.alloc([128, 8, 128], F32)
        acs_sb = A.alloc([128, 8], F32)
        eacs = A.alloc([128, 8], F32)
        tmp8 = A.alloc([128, 8], F32)
        last8 = A.alloc([128, 8], F32)
        decs = A.alloc([128, 8], F32)
        cdB = A.alloc([128, 8], F32)
        w8 = A.alloc([128, 8], F32)
        CBm = A.alloc([128, 128], F32)
        dif = A.alloc([128, 8, 128], F32)
        MT = A.alloc([128, 8, 128], BF16)
        xdt = A.alloc([128, 8, 64], BF16)
        xdec = A.alloc([128, 8, 64], BF16)
        t1 = A.alloc([128, 8, 64], F32)
        yb = A.alloc([128, 8, 64], F32)
        t2 = A.alloc([128, 8, 64], F32)
        ygn = A.alloc([128, 512], BF16)
        ysq = A.alloc([128, 512], BF16)
        yss = A.alloc([128, 2], F32)
        Sst = A.alloc([128, 8, 64], F32)
        S_bf = A.alloc([128, 512], BF16)
        ygT = A.alloc([128, 4, 384], BF16)

        P.op("dve", lambda e: e.memset(vst, 1.0), writes=["vst"])
        P.op("dve", lambda e: e.memset(xraw, 0.0), writes=[("xraw", t) for t in range(6)])
        P.op("dve", lambda e: e.memset(Sst, 0.0), writes=["S"])
        P.op("dve", lambda e: e.memset(S_bf, 0.0), writes=["S_bf"])

        proj_rot = [0]

        def proj_bank():
            b = proj_rot[0] % 2
            proj_rot[0] += 1
            return b

        xslot = [0]
        for blk in range(NB1):
            t0 = blk * 384
            P.op("dve", lambda e: e.memset(ss3, 0.0), writes=["ss3"])
            for c in range(3):
                s = xslot[0] % 2
                xslot[0] += 1
                dma("sp", xt[s], xp[t0 + c * 128:t0 + (c + 1) * 128, :], writes=[("xt", s)])
                P.op("act", lambda e, s=s, c=c: e.activation(out=xn[s], in_=xt[s], func=AF.Square,
                                                             accum_out=ss3[:, c:c + 1]),
                     reads=[("xt", s), "ss3"], writes=[("xn", s), "ss3"])
                P.op("dve", lambda e, c=c: e.tensor_scalar(out=rs3[:, c:c + 1], in0=ss3[:, c:c + 1],
                                                           scalar1=1.0 / D, scalar2=EPS, op0=ALU.mult,
                                                           op1=ALU.add), reads=["ss3"], writes=["rs3"])
                P.op("act", lambda e, c=c: e.sqrt(out=rs3[:, c:c + 1], in_=rs3[:, c:c + 1]), reads=["rs3"], writes=["rs3"])
                P.op("dve", lambda e, c=c: e.reciprocal(out=rs3[:, c:c + 1], in_=rs3[:, c:c + 1]), reads=["rs3"], writes=["rs3"])
                P.op("act", lambda e, s=s, c=c: e.activation(out=xn[s], in_=xt[s], func=AF.Copy,
                                                             scale=rs3[:, c:c + 1]),
                     reads=[("xt", s), "rs3"], writes=[("xn", s)])
                for half in range(2):
                    for k8 in range(8):
                        kc = half * 8 + k8
                        P.op("pe", lambda e, s=s, kc=kc, k8=k8: e.transpose(
                            out=psb(2, 128, off=k8 * 128), in_=xn[s][:, kc * 128:(kc + 1) * 128],
                            identity=identb), reads=[("xn", s), "identb"], writes=[("ps", 2)])
                    P.op("dve", lambda e, c=c, half=half: e.tensor_copy(
                        out=uT[:, half * 8:(half + 1) * 8, c * 128:(c + 1) * 128],
                        in_=psb(2, 1024).rearrange("p (a b) -> p a b", a=8)),
                        reads=[("ps", 2)], writes=["uT"])
            if STOP == 10:
                break
            for gi in range(14):
                if gi < 8:
                    c0, M = gi * 64, 64
                else:
                    c0, M = 1536 + (gi - 8) * 128, 128
                b = proj_bank()
                for kc in range(KC):
                    P.op("pe", lambda e, b=b, c0=c0, M=M, kc=kc: e.matmul(
                        psf(b, 384, parts=M), lhsT=W1sb[:, kc, c0:c0 + M], rhs=uT[:, kc, :],
                        start=(kc == 0), stop=(kc == KC - 1)),
                        reads=[("W1", kc), "uT"], writes=[("ps", b)])
                if gi < 4:
                    P.op("act", lambda e, b=b, gi=gi: e.activation(out=qkst[0:64, gi, :],
                                                                   in_=psf(b, 384, parts=64),
                                                                   func=AF.Copy, scale=0.125),
                         reads=[("ps", b)], writes=[("qkst", gi)])
                    hl, m = gi // 2, gi % 2
                    dma("sp", Qd[hl, m, 0:64, t0:t0 + 384], qkst[0:64, gi, :], reads=[("qkst", gi)],
                        writes=[("Qd", hl, m)])
                elif gi < 8:
                    P.op("act", lambda e, b=b, gi=gi: e.copy(out=qkst[0:64, gi, :],
                                                             in_=psf(b, 384, parts=64)),
                         reads=[("ps", b)], writes=[("qkst", gi)])
                    hl, m = (gi - 4) // 2, gi % 2
                    dma("sp", Kd[hl, m, 0:64, t0:t0 + 384], qkst[0:64, gi, :], reads=[("qkst", gi)],
                        writes=[("Kd", hl, m)])
                else:
                    t = gi - 8
                    if True:
                        P.op("pool", lambda e, t=t: e.tensor_copy(out=xraw[:, t, 0:3],
                                                                  in_=xraw[:, t, 384:387]),
                             reads=[("xraw", t)], writes=[("xraw", t)])
                    P.op("act", lambda e, b=b, t=t: e.copy(out=xraw[:, t, 3:387], in_=psf(b, 384)),
                         reads=[("ps", b)], writes=[("xraw", t)])
                    ab = accb[t % 2]
                    ak = ("acc", t % 2)
                    P.op("pool", lambda e, t=t, ab=ab: e.tensor_scalar(
                        out=ab, in0=xraw[:, t, 0:384], scalar1=convw_s[:, t, 0:1], scalar2=None,
                        op0=ALU.mult), reads=[("xraw", t), "convw"], writes=[ak])
                    for j in range(1, 4):
                        P.op("pool", lambda e, t=t, j=j: e.tensor_scalar(
                            out=ctmp, in0=xraw[:, t, j:j + 384], scalar1=convw_s[:, t, j:j + 1],
                            scalar2=None, op0=ALU.mult), reads=[("xraw", t), "convw"], writes=["ctmp"])
                        P.op("pool", lambda e, ab=ab: e.tensor_tensor(out=ab, in0=ab, in1=ctmp, op=ALU.add),
                             reads=["ctmp", ak], writes=[ak])
                    P.op("act", lambda e, t=t, ab=ab: e.activation(out=xc[:, t, :], in_=ab, func=AF.Silu,
                                                                   bias=convb_s[:, t:t + 1]),
                         reads=[ak, "convb"], writes=[("xc", t)])
                    if blk == 0:
                        P.op("pool", lambda e, t=t: e.memset(xc[:, t, 0:PAD], 0.0), writes=[("xc", t)])
            if STOP == 11:
                break
            for c in range(3):
                for gname, c0, N in (("vg", 512, 512), ("z", 1024, 512), ("dt", 2280, 32)):
                    import os as _os
                    KV = int(_os.environ.get("KV", "127"))
                    bit = {"vg": 1, "z": 4, "dt": 16}[gname]
                    if not (KV & bit):
                        continue
                    b = proj_bank()
                    for kc in range(KC):
                        P.op("pe", lambda e, b=b, c0=c0, N=N, kc=kc, c=c: e.matmul(
                            psf(b, N), lhsT=uT[:, kc, c * 128:(c + 1) * 128],
                            rhs=W1sb[:, kc, c0:c0 + N], start=(kc == 0), stop=(kc == KC - 1)),
                            reads=[("W1", kc), "uT"], writes=[("ps", b)])
                    if not (KV & (bit * 2)):
                        continue
                    if gname == "vg":
                        P.op("dve", lambda e, b=b, c=c: e.tensor_copy(
                            out=vst[:, c, :, 0:128],
                            in_=psf(b, 256).rearrange("p (h e) -> p h e", h=2)),
                            reads=[("ps", b)], writes=["vst"])
                        P.op("dve", lambda e, b=b, c=c: e.tensor_copy(out=gst[:, c, :],
                                                                      in_=psf(b, 256, off=256)),
                             reads=[("ps", b)], writes=["gst"])
                        P.op("act", lambda e, c=c: e.activation(out=gst[:, c, :], in_=gst[:, c, :],
                                                                func=AF.Silu),
                             reads=["gst"], writes=["gst"])
                    elif gname == "z":
                        P.op("act", lambda e, b=b, c=c: e.activation(out=zs[:, c, :], in_=psf(b, 512),
                                                                     func=AF.Silu),
                             reads=[("ps", b)], writes=[("zs", c)])
                    else:
                        P.op("dve", lambda e, b=b, c=c: e.tensor_tensor(out=dtr[:, c, :], in0=psf(b, 8, off=24),
                                                                        in1=dtb_s, op=ALU.add),
                             reads=[("ps", b), "dtb"], writes=["dtr"])
            if STOP == 121:
                break
            for hl in range(2):
                dma("sp", Vd[hl, 3 * blk:3 * blk + 3].rearrange("c p e -> p c e"), vst[:, :, hl, :],
                    reads=["vst"], writes=[("Vd", hl)])
            dma("sp", Gd[t0:t0 + 384, :].rearrange("(c p) e -> p c e", p=128), gst, reads=["gst"],
                writes=["Gd"])
            if STOP == 122:
                break
            P.op("dve", lambda e: e.tensor_scalar(out=dtt, in0=dtr, scalar1=-1.0, scalar2=None, op0=ALU.mult),
                 reads=["dtr"], writes=["dtt"])
            P.op("dve", lambda e: e.tensor_tensor(out=dtt, in0=dtt, in1=dtr, op=ALU.max),
                 reads=["dtr", "dtt"], writes=["dtt"])
            P.op("act", lambda e: e.activation(out=dte, in_=dtt, func=AF.Exp, scale=-1.0),
                 reads=["dtt"], writes=["dte"])
            P.op("act", lambda e: e.activation(out=dte, in_=dte, func=AF.Ln, bias=1.0),
                 reads=["dte"], writes=["dte"])
            P.op("dve", lambda e: e.scalar_tensor_tensor(out=dtv, in0=dtr, scalar=0.0, in1=dte,
                                                         op0=ALU.max, op1=ALU.add),
                 reads=["dtr", "dte"], writes=["dtv"])
            if blk == 0:
                P.op("dve", lambda e: e.memset(dtv[0:PAD, 0, :], 0.0), writes=["dtv"])

            if STOP == 12:
                break
            for c in range(3):
                cg = 3 * blk + c
                cs = slice(c * 128, (c + 1) * 128)
                for t in range(5):
                    bank, off = (4, t * 128) if t < 4 else (5, 0)
                    P.op("pe", lambda e, t=t, bank=bank, off=off, cs=cs: e.transpose(
                        out=psf(bank, 128, off=off), in_=xc[:, t, cs], identity=identf),
                        reads=[("xc", t), "identf"], writes=[("ps", bank)])
                P.op("act", lambda e: e.copy(out=xs_tm.rearrange("p h d -> p (h d)"), in_=psf(4, 512)),
                     reads=[("ps", 4)], writes=["xs_tm"])
                P.op("dve", lambda e: e.tensor_copy(out=B_bf, in_=psf(5, 128)), reads=[("ps", 5)],
                     writes=["B_bf"])
                P.op("pool", lambda e, cs=cs: e.tensor_copy(out=BT_bf, in_=xc[:, 4, cs]),
                     reads=[("xc", 4)], writes=["BT_bf"])
                P.op("pool", lambda e, cs=cs: e.tensor_copy(out=CT_bf, in_=xc[:, 5, cs]),
                     reads=[("xc", 5)], writes=["CT_bf"])
                P.op("dve", lambda e, c=c: e.tensor_tensor(out=da, in0=dtv[:, c, :], in1=nega, op=ALU.mult),
                     reads=["dtv", "nega"], writes=["da"])
                P.op("pool", lambda e: e.tensor_tensor(
                    out=daU, in0=umat.unsqueeze(1).broadcast_to([128, 8, 128]),
                    in1=da.unsqueeze(2).broadcast_to([128, 8, 128]), op=ALU.mult),
                    reads=["da", "umat"], writes=["daU"])
                P.op("pe", lambda e: e.matmul(psf(3, 8), lhsT=umat, rhs=da, start=True, stop=True),
                     reads=["umat", "da"], writes=[("ps", 3)])
                for hf in range(2):
                    P.op("pe", lambda e, hf=hf: e.matmul(
                        psf(4 + hf, 512), lhsT=onesf,
                        rhs=daU[:, hf * 4:(hf + 1) * 4, :].rearrange("p h l -> p (h l)"),
                        start=True, stop=True), reads=["onesf", "daU"], writes=[("ps", 4 + hf)])
                P.op("pe", lambda e: e.matmul(psf(3, 128, off=128), lhsT=BT_bf, rhs=CT_bf, start=True,
                                              stop=True), reads=["BT_bf", "CT_bf"],
                     writes=[("ps", 3)])
                P.op("dve", lambda e: e.tensor_copy(out=acs_sb, in_=psf(3, 8)), reads=[("ps", 3)],
                     writes=["acs_sb"])
                P.op("act", lambda e: e.activation(out=eacs, in_=acs_sb, func=AF.Exp), reads=["acs_sb"],
                     writes=["eacs"])
                for hf in range(2):
                    lastv = psf(4 + hf, 512).rearrange("p (h l) -> p h l", h=4)[:, :, 127]
                    P.op("dve", lambda e, hf=hf, lastv=lastv: e.tensor_copy(
                        out=last8[:, hf * 4:(hf + 1) * 4], in_=lastv), reads=[("ps", 4 + hf)],
                        writes=["last8"])
                P.op("dve", lambda e: e.tensor_tensor(out=tmp8, in0=last8, in1=acs_sb, op=ALU.subtract),
                     reads=["last8", "acs_sb"], writes=["tmp8"])
                P.op("act", lambda e: e.activation(out=cdB, in_=last8, func=AF.Exp), reads=["last8"],
                     writes=["cdB"])
                P.op("act", lambda e: e.activation(out=decs, in_=tmp8, func=AF.Exp), reads=["tmp8"],
                     writes=["decs"])
                P.op("dve", lambda e: e.tensor_tensor(out=CBm, in0=psf(3, 128, off=128), in1=umat,
                                                      op=ALU.mult), reads=[("ps", 3), "umat"],
                     writes=["CBm"])
                for h in range(8):
                    bank, hh = 4 + h // 4, h % 4
                    P.op("dve", lambda e, h=h, bank=bank, hh=hh: e.scalar_tensor_tensor(
                        out=dif[:, h, :], in0=psf(bank, 128, off=hh * 128), scalar=acs_sb[:, h:h + 1],
                        in1=mneg, op0=ALU.subtract, op1=ALU.add),
                        reads=[("ps", bank), "acs_sb", "mneg"], writes=["dif"])
                P.op("act", lambda e: e.activation(out=dif, in_=dif, func=AF.Exp), reads=["dif"],
                     writes=["dif"])
                P.op("pool", lambda e: e.tensor_tensor(
                    out=MT, in0=dif, in1=CBm.unsqueeze(1).broadcast_to([128, 8, 128]), op=ALU.mult),
                    reads=["dif", "CBm"], writes=["MT"])
                P.op("pool", lambda e, c=c: e.tensor_tensor(
                    out=xdt, in0=xs_tm, in1=dtv[:, c, :].unsqueeze(2).broadcast_to([128, 8, 64]),
                    op=ALU.mult), reads=["xs_tm", "dtv"], writes=["xdt"])
                P.op("dve", lambda e, c=c: e.tensor_tensor(out=w8, in0=dtv[:, c, :], in1=decs, op=ALU.mult),
                     reads=["dtv", "decs"], writes=["w8"])
                P.op("pool", lambda e: e.tensor_tensor(
                    out=xdec, in0=xs_tm, in1=w8.unsqueeze(2).broadcast_to([128, 8, 64]), op=ALU.mult),
                    reads=["xs_tm", "w8"], writes=["xdec"])
                for h in range(8):
                    P.op("pe", lambda e, h=h: e.matmul(psf(6, 64, off=h * 64), lhsT=MT[:, h, :],
                                                       rhs=xdt[:, h, :], start=True, stop=True),
                         reads=["MT", "xdt"], writes=[("ps", 6)])
                P.op("pe", lambda e: e.matmul(psf(7, 512), lhsT=CT_bf, rhs=S_bf, start=True, stop=True),
                     reads=["CT_bf", "S_bf"], writes=[("ps", 7)])
                if cg >= 1:
                    P.op("dve", lambda e: e.tensor_tensor(
                        out=t1, in0=psf(7, 512).rearrange("p (h d) -> p h d", h=8),
                        in1=eacs.unsqueeze(2).broadcast_to([128, 8, 64]), op=ALU.mult),
                        reads=[("ps", 7), "eacs"], writes=["t1"])
                    P.op("dve", lambda e: e.tensor_tensor(
                        out=yb, in0=psf(6, 512).rearrange("p (h d) -> p h d", h=8), in1=t1, op=ALU.add),
                        reads=[("ps", 6), "t1"], writes=["yb"])
                    P.op("pool", lambda e: e.tensor_tensor(out=t2, in0=xs_tm, in1=dskB, op=ALU.mult),
                         reads=["xs_tm", "dskB"], writes=["t2"])
                    P.op("pool", lambda e: e.tensor_tensor(out=yb, in0=yb, in1=t2, op=ALU.add),
                         reads=["yb", "t2"], writes=["yb"])
                    P.op("pool", lambda e, c=c: e.tensor_tensor(
                        out=yb, in0=yb, in1=zs[:, c, :].rearrange("p (h d) -> p h d", h=8), op=ALU.mult),
                        reads=["yb", ("zs", c)], writes=["yb"])
                    P.op("dve", lambda e: e.memset(yss[:, 0:1], 0.0), writes=["yss"])
                    P.op("act", lambda e: e.activation(out=ysq, in_=yb.rearrange("p h d -> p (h d)"),
                                                       func=AF.Square, accum_out=yss[:, 0:1]),
                         reads=["yb", "yss"], writes=["ysq", "yss"])
                    P.op("dve", lambda e: e.tensor_scalar(out=yss[:, 1:2], in0=yss[:, 0:1],
                                                          scalar1=1.0 / 512, scalar2=EPS, op0=ALU.mult,
                                                          op1=ALU.add), reads=["yss"], writes=["yss1"])
                    P.op("act", lambda e: e.sqrt(out=yss[:, 1:2], in_=yss[:, 1:2]), reads=["yss1"], writes=["yss1"])
                    P.op("dve", lambda e: e.reciprocal(out=yss[:, 1:2], in_=yss[:, 1:2]), reads=["yss1"], writes=["yss1"])
                    P.op("act", lambda e: e.activation(out=ygn, in_=yb.rearrange("p h d -> p (h d)"),
                                                       func=AF.Copy, scale=yss[:, 1:2]),
                         reads=["yb", "yss1"], writes=["ygn"])
                    for t in range(4):
                        P.op("pe", lambda e, t=t: e.transpose(out=psb(2, 128, off=t * 128),
                                                              in_=ygn[:, t * 128:(t + 1) * 128],
                                                              identity=identb),
                             reads=["ygn", "identb"], writes=[("ps", 2)])
                    P.op("dve", lambda e, cs=cs: e.tensor_copy(
                        out=ygT[:, :, cs], in_=psb(2, 512).rearrange("p (t k) -> p t k", t=4)),
                        reads=[("ps", 2)], writes=["ygT"])
                P.op("pe", lambda e: e.matmul(psf(6, 512), lhsT=B_bf, rhs=xdec.rearrange("p h d -> p (h d)"),
                                              start=True, stop=True), reads=["B_bf", "xdec"],
                     writes=[("ps", 6)])
                P.op("dve", lambda e: e.tensor_tensor(
                    out=Sst, in0=Sst, in1=cdB.unsqueeze(2).broadcast_to([128, 8, 64]), op=ALU.mult),
                    reads=["S", "cdB"], writes=["S"])
                P.op("dve", lambda e: e.tensor_tensor(
                    out=Sst, in0=Sst, in1=psf(6, 512).rearrange("p (h d) -> p h d", h=8), op=ALU.add),
                    reads=["S", ("ps", 6)], writes=["S"])
                P.op("act", lambda e: e.copy(out=S_bf, in_=Sst.rearrange("p h d -> p (h d)")),
                     reads=["S"], writes=["S_bf"])
            lo = 1 if blk == 0 else 0
            tok0 = t0 - 128 + lo * 128
            ncols = (3 - lo) * 128
            cur = tok0
            while cur < tok0 + ncols:
                pi = cur // PS
                hi = min(tok0 + ncols, (pi + 1) * PS)
                src0 = lo * 128 + (cur - tok0)
                dma("sp", cin_s.ap()[pi, :, cur - pi * PS:hi - pi * PS].rearrange("(t p) k -> p t k", p=128),
                    ygT[:, :, src0:src0 + (hi - cur)], reads=["ygT"], writes=["cin_s"])
                cur = hi
        P.barrier()
        if STOP in (1, 10, 11, 12, 13, 14, 121, 122):
            P.emit(st)
            return nc
        if CC:
            for pi in range(NP):
                P.op("pool", lambda e, pi=pi: e.collective_compute(
                    "AllGather", ALU.bypass, replica_groups=RG, ins=[cin_s.ap()[pi]],
                    outs=[cout_s.ap()[pi]]), reads=["cin_s"], writes=["cout_s"], cc=True)
        A.reset(m_persist)
        if STOP == 15:
            P.barrier()
            P.emit(st)
            return nc

        kT = [A.alloc([66, L], BF16) for _ in range(2)]
        Vsb = A.alloc([128, NCH, 144], BF16)
        qT = [[A.alloc([66, 512], BF16) for _ in range(2)] for _ in range(2)]
        Gb = [A.alloc([128, 4, 128], F32) for _ in range(2)]
        gg = A.alloc([128, 4, 128], F32)
        PT = [A.alloc([128, 512], BF16) for _ in range(4)]
        rr = A.alloc([128, 4, 2], F32)
        tt = A.alloc([128, 128], F32)
        ob = A.alloc([128, 4, 128], F32)
        osq = A.alloc([128, 128], BF16)
        oss = A.alloc([128, 4, 2], F32)
        onb = A.alloc([128, 4, 128], BF16)
        oaT = A.alloc([128, 512], BF16)

        def oacc(a, m):
            i = a * 2 + m
            return psf(i // 3, 129, off=(i % 3) * 129), ("ps", i // 3)

        srot = [0]
        for hl in range(2):
            for m in range(2):
                for c_lo in range(0, NCH, 16):
                    c_hi = min(NCH, c_lo + 16)
                    dma("sp", kT[m][:, c_lo * 128:c_hi * 128], Kd[hl, m, :, c_lo * 128:c_hi * 128],
                        reads=[("Kd", hl, m)], writes=[("kT", m)])
            for c_lo in range(0, NCH, 4):
                c_hi = min(NCH, c_lo + 4)
                dma("sp", Vsb[:, c_lo:c_hi, :], Vd[hl, c_lo:c_hi].rearrange("c p e -> p c e"),
                    reads=[("Vd", hl)], writes=["Vsb"])
            for qb in range(NQB):
                c0 = 1 + 4 * qb
                q0 = c0 * 128
                par = qb % 2
                for m in range(2):
                    dma("sp", qT[par][m], Qd[hl, m, :, q0:q0 + 512], reads=[("Qd", hl, m)],
                        writes=[("qT", par, m)])
                dma("sp", Gb[par], Gd[q0:q0 + 512, hl * 128:(hl + 1) * 128].rearrange(
                    "(a p) e -> p a e", p=128), reads=["Gd"], writes=[("Gb", par)])
                started = set()
                for m in range(2):
                    for kc in range(c0 + 4):
                        amin = max(0, kc - c0)
                        n0 = amin * 128
                        sb_ = 3 + srot[0] % 4
                        pt = PT[srot[0] % 4]
                        ptk = ("PT", srot[0] % 4)
                        srot[0] += 1
                        P.op("pe", lambda e, sb_=sb_, m=m, kc=kc, n0=n0, par=par: e.matmul(
                            psf(sb_, 512 - n0, off=n0), lhsT=kT[m][:, kc * 128:(kc + 1) * 128],
                            rhs=qT[par][m][:, n0:512], start=True, stop=True),
                            reads=[("kT", m), ("qT", par, m)], writes=[("ps", sb_)])
                        bias = btab0[:, hl, qb:qb + 1] if kc == 0 else \
                            btab[:, hl, c0 - kc + 3:c0 - kc + 4]
                        P.op("act", lambda e, sb_=sb_, pt=pt, n0=n0, bias=bias: e.activation(
                            out=pt[:, n0:512], in_=psf(sb_, 512 - n0, off=n0), func=AF.Exp, bias=bias),
                            reads=[("ps", sb_), "btab", "btab0"], writes=[ptk])
                        if kc >= c0:
                            P.op("pool", lambda e, pt=pt, n0=n0: e.tensor_tensor(
                                out=pt[:, n0:n0 + 128], in0=pt[:, n0:n0 + 128], in1=tri, op=ALU.mult),
                                reads=[ptk, "tri"], writes=[ptk])
                        for a in range(amin, 4):
                            oap, okey = oacc(a, m)
                            first = okey not in started
                            started.add(okey)
                            P.op("pe", lambda e, oap=oap, pt=pt, a=a, kc=kc, c0=c0, first=first: e.matmul(
                                oap, lhsT=pt[:, a * 128:(a + 1) * 128], rhs=Vsb[:, kc, 0:129],
                                start=first, stop=(kc == c0 + a), skip_group_check=True),
                                reads=[ptk, "Vsb"], writes=[okey])
                P.op("dve", lambda e, par=par: e.tensor_tensor(
                    out=gg, in0=Gb[par], in1=subln_s.unsqueeze(1).broadcast_to([128, 4, 128]),
                    op=ALU.mult), reads=[("Gb", par), "subln"], writes=["gg"])
                P.op("dve", lambda e: e.memset(oss[:, :, 0:1], 0.0), writes=["oss"])
                for a in range(4):
                    o1, k1 = oacc(a, 0)
                    o2, k2 = oacc(a, 1)
                    P.op("dve", lambda e, a=a, o1=o1: e.reciprocal(out=rr[:, a, 0:1], in_=o1[:, 128:129]),
                         reads=[k1], writes=["rr"])
                    P.op("dve", lambda e, a=a, o2=o2: e.reciprocal(out=rr[:, a, 1:2], in_=o2[:, 128:129]),
                         reads=[k2], writes=["rr"])
                    P.op("dve", lambda e, a=a: e.tensor_tensor(out=rr[:, a, 1:2], in0=rr[:, a, 1:2],
                                                               in1=neglam, op=ALU.mult),
                         reads=["rr", "neglam"], writes=["rr"])
                    P.op("dve", lambda e, a=a, o1=o1: e.tensor_scalar(
                        out=tt, in0=o1[:, 0:128], scalar1=rr[:, a, 0:1], scalar2=None, op0=ALU.mult),
                        reads=[k1, "rr"], writes=["tt"])
                    P.op("dve", lambda e, a=a, o2=o2: e.scalar_tensor_tensor(
                        out=ob[:, a, :], in0=o2[:, 0:128], scalar=rr[:, a, 1:2], in1=tt, op0=ALU.mult,
                        op1=ALU.add), reads=[k2, "rr", "tt"], writes=[("ob", a)])
                    P.op("act", lambda e, a=a: e.activation(out=osq, in_=ob[:, a, :], func=AF.Square,
                                                            accum_out=oss[:, a, 0:1]),
                         reads=[("ob", a), "oss"], writes=["osq", "oss"])
                    P.op("dve", lambda e, a=a: e.tensor_scalar(
                        out=oss[:, a, 1:2], in0=oss[:, a, 0:1], scalar1=1.0 / 128, scalar2=EPS,
                        op0=ALU.mult, op1=ALU.add), reads=["oss"], writes=[("oss1", a)])
                    P.op("act", lambda e, a=a: e.sqrt(out=oss[:, a, 1:2], in_=oss[:, a, 1:2]), reads=[("oss1", a)], writes=[("oss1", a)])
                    P.op("dve", lambda e, a=a: e.reciprocal(out=oss[:, a, 1:2], in_=oss[:, a, 1:2]), reads=[("oss1", a)], writes=[("oss1", a)])
                    P.op("dve", lambda e, a=a: e.scalar_tensor_tensor(
                        out=onb[:, a, :], in0=ob[:, a, :], scalar=oss[:, a, 1:2], in1=gg[:, a, :],
                        op0=ALU.mult, op1=ALU.mult), reads=[("ob", a), ("oss1", a), "gg"],
                        writes=[("onb", a)])
                    P.op("pe", lambda e, a=a: e.transpose(out=psb(7, 128, off=a * 128), in_=onb[:, a, :],
                                                          identity=identb),
                         reads=[("onb", a), "identb"], writes=[("ps", 7)])
                P.op("act", lambda e: e.copy(out=oaT, in_=psb(7, 512)), reads=[("ps", 7)],
                     writes=["oaT"])
                qt = q0 - 128
                dma("sp", cin_a.ap()[qt // PS, hl * 128:(hl + 1) * 128, qt % PS:qt % PS + 512], oaT,
                    reads=["oaT"], writes=["cin_a"])
        P.barrier()
        if STOP == 2:
            P.emit(st)
            return nc
        for pi in range(NP):
            P.op("pool", lambda e, pi=pi: e.collective_compute(
                "AllGather", ALU.bypass, replica_groups=RG, ins=[cin_a.ap()[pi]],
                outs=[cout_a.ap()[pi]]), reads=["cin_a"], writes=["cout_a"], cc=True)
        A.reset(m_persist)

        finalg = A.alloc([128, D], F32)
        wos = [A.alloc([128, KC, 512], BF16) for _ in range(2)]
        w3s = [A.alloc([128, 56, 128], BF16) for _ in range(2)]
        hout = A.alloc([128, NT3, D], F32)
        xn3 = [A.alloc([128, D], BF16) for _ in range(2)]
        uT3 = A.alloc([128, KC, T3], BF16)
        oaT3 = A.alloc([128, 8, T3], BF16)
        ygT3 = A.alloc([128, 16, T3], BF16)
        mgT = A.alloc([128, KC, T3], BF16)
        sga = A.alloc([128, T3], F32)
        sgs = A.alloc([128, T3], F32)
        ss4 = A.alloc([128, 8], F32)
        rs4 = A.alloc([128, 8], F32)
        hsq = A.alloc([128, D], BF16)
        dma("sp", finalg, finalg_i, writes=["finalg"])
        cs_view = cout_s.reshape([NP * 2048 * CPP, T3]).ap()
        ca_view = cout_a.reshape([NP * 1024 * CPP, T3]).ap()
        w3rot = [0]
        worot = [0]
        for tb in range(NTB):
            for rk in range(4):
                for ft in range(4):
                    col = (rk * 4 + ft) * NTB + tb
                    P.op("pool", lambda e, rk=rk, ft=ft, col=col: e.indirect_dma_start(
                        out=ygT3[:, rk * 4 + ft, :], out_offset=None, in_=cs_view,
                        in_offset=bass.IndirectOffsetOnAxis(ap=idxs[:, col:col + 1], axis=0)),
                        reads=["cout_s", "idxs"], writes=["ygT3"], dma=True)
                for ft in range(2):
                    col = (rk * 2 + ft) * NTB + tb
                    P.op("pool", lambda e, rk=rk, ft=ft, col=col: e.indirect_dma_start(
                        out=oaT3[:, rk * 2 + ft, :], out_offset=None, in_=ca_view,
                        in_offset=bass.IndirectOffsetOnAxis(ap=idxa[:, col:col + 1], axis=0)),
                        reads=["cout_a", "idxa"], writes=["oaT3"], dma=True)
            P.op("dve", lambda e: e.memset(ss4, 0.0), writes=["ss4"])
            for i in range(NT3):
                ti = tb * NT3 + i
                P.op("pool", lambda e, i=i, ti=ti: e.indirect_dma_start(
                    out=hout[:, i, :], out_offset=None, in_=xp,
                    in_offset=bass.IndirectOffsetOnAxis(ap=idxx[:, ti:ti + 1], axis=0)),
                    reads=["idxx"], writes=[("hout", i)], dma=True)
                s = i % 2
                P.op("act", lambda e, s=s, i=i: e.activation(out=xn3[s], in_=hout[:, i, :], func=AF.Square,
                                                             accum_out=ss4[:, i:i + 1]),
                     reads=[("hout", i), "ss4"], writes=[("xn3", s), "ss4"])
                P.op("dve", lambda e, i=i: e.tensor_scalar(out=rs4[:, i:i + 1], in0=ss4[:, i:i + 1],
                                                           scalar1=1.0 / D, scalar2=EPS, op0=ALU.mult,
                                                           op1=ALU.add), reads=["ss4"], writes=["rs4"])
                P.op("act", lambda e, i=i: e.sqrt(out=rs4[:, i:i + 1], in_=rs4[:, i:i + 1]), reads=["rs4"], writes=["rs4"])
                P.op("dve", lambda e, i=i: e.reciprocal(out=rs4[:, i:i + 1], in_=rs4[:, i:i + 1]), reads=["rs4"], writes=["rs4"])
                P.op("act", lambda e, s=s, i=i: e.activation(out=xn3[s], in_=hout[:, i, :], func=AF.Copy,
                                                             scale=rs4[:, i:i + 1]),
                     reads=[("hout", i), "rs4"], writes=[("xn3", s)])
                for half in range(2):
                    for k8 in range(8):
                        kc = half * 8 + k8
                        P.op("pe", lambda e, s=s, kc=kc, k8=k8: e.transpose(
                            out=psb(6, 128, off=k8 * 128), in_=xn3[s][:, kc * 128:(kc + 1) * 128],
                            identity=identb), reads=[("xn3", s), "identb"], writes=[("ps", 6)])
                    P.op("dve", lambda e, i=i, half=half: e.tensor_copy(
                        out=uT3[:, half * 8:(half + 1) * 8, i * 128:(i + 1) * 128],
                        in_=psb(6, 1024).rearrange("p (a b) -> p a b", a=8)),
                        reads=[("ps", 6)], writes=["uT3"])
            for j in range(16):
                ws_ = w3rot[0] % 2
                w3rot[0] += 1
                dma("sp", w3s[ws_], W3d[j], reads=["W3d"], writes=[("w3s", ws_)])
                for (bank, s0, nk, rhs_t, rk_) in ((0, 0, 16, uT3, "uT3"), (1, 16, 16, uT3, "uT3"),
                                                   (2, 32, 8, oaT3, "oaT3"), (3, 40, 16, ygT3, "ygT3")):
                    for kc in range(nk):
                        P.op("pe", lambda e, bank=bank, s0=s0, kc=kc, nk=nk, rhs_t=rhs_t, ws_=ws_: e.matmul(
                            psf(bank, T3), lhsT=w3s[ws_][:, s0 + kc, :], rhs=rhs_t[:, kc, :],
                            start=(kc == 0), stop=(kc == nk - 1)),
                            reads=[("w3s", ws_), rk_], writes=[("ps", bank)])
                P.op("act", lambda e: e.activation(out=sga, in_=psf(0, T3), func=AF.Sigmoid),
                     reads=[("ps", 0)], writes=["sga"])
                P.op("act", lambda e: e.activation(out=sgs, in_=psf(1, T3), func=AF.Sigmoid),
                     reads=[("ps", 1)], writes=["sgs"])
                P.op("dve", lambda e: e.tensor_tensor(out=sga, in0=sga, in1=psf(2, T3), op=ALU.mult),
                     reads=["sga", ("ps", 2)], writes=["sga"])
                P.op("dve", lambda e: e.tensor_tensor(out=sgs, in0=sgs, in1=psf(3, T3), op=ALU.mult),
                     reads=["sgs", ("ps", 3)], writes=["sgs"])
                P.op("pool", lambda e, j=j: e.tensor_tensor(out=mgT[:, j, :], in0=sga, in1=sgs, op=ALU.add),
                     reads=["sga", "sgs"], writes=["mgT"])
            for n4 in range(4):
                wsl = worot[0] % 2
                worot[0] += 1
                dma("sp", wos[wsl], Wod[:, :, n4 * 512:(n4 + 1) * 512], reads=["Wod"],
                    writes=[("wos", wsl)])
                for i in range(NT3):
                    bank = 4 + i % 2
                    for kc in range(KC):
                        P.op("pe", lambda e, bank=bank, kc=kc, i=i, wsl=wsl: e.matmul(
                            psf(bank, 512), lhsT=mgT[:, kc, i * 128:(i + 1) * 128], rhs=wos[wsl][:, kc, :],
                            start=(kc == 0), stop=(kc == KC - 1)),
                            reads=["mgT", ("wos", wsl)], writes=[("ps", bank)])
                    P.op("dve", lambda e, bank=bank, i=i, n4=n4: e.tensor_tensor(
                        out=hout[:, i, n4 * 512:(n4 + 1) * 512], in0=hout[:, i, n4 * 512:(n4 + 1) * 512],
                        in1=psf(bank, 512), op=ALU.add), reads=[("hout", i), ("ps", bank)],
                        writes=[("hout", i)])
            P.op("dve", lambda e: e.memset(ss4, 0.0), writes=["ss4"])
            for i in range(NT3):
                ti = tb * NT3 + i
                P.op("act", lambda e, i=i: e.activation(out=hsq, in_=hout[:, i, :], func=AF.Square,
                                                        accum_out=ss4[:, i:i + 1]),
                     reads=[("hout", i), "ss4"], writes=["hsq", "ss4"])
                P.op("dve", lambda e, i=i: e.tensor_scalar(out=rs4[:, i:i + 1], in0=ss4[:, i:i + 1],
                                                           scalar1=1.0 / D, scalar2=EPS, op0=ALU.mult,
                                                           op1=ALU.add), reads=["ss4"], writes=["rs4"])
                P.op("act", lambda e, i=i: e.sqrt(out=rs4[:, i:i + 1], in_=rs4[:, i:i + 1]), reads=["rs4"], writes=["rs4"])
                P.op("dve", lambda e, i=i: e.reciprocal(out=rs4[:, i:i + 1], in_=rs4[:, i:i + 1]), reads=["rs4"], writes=["rs4"])
                P.op("dve", lambda e, i=i: e.scalar_tensor_tensor(
                    out=hout[:, i, :], in0=hout[:, i, :], scalar=rs4[:, i:i + 1], in1=finalg,
                    op0=ALU.mult, op1=ALU.mult), reads=[("hout", i), "rs4", "finalg"],
                    writes=[("hout", i)])
                dma("sp", out[ti * 128:(ti + 1) * 128, :], hout[:, i, :], reads=[("hout", i)],
                    writes=["out"])
        P.barrier()
        P.emit(st)
    return nc


Q_OFF, K_OFF, V_OFF, G_OFF, Z_OFF, XBC_OFF, DT_OFF, MA_OFF = 0, 1024, 2048, 3072, 4096, 6144, 9216, 9248


def _bf(a):
    return np.ascontiguousarray(a).astype(ml_dtypes.bfloat16)


def make_in_maps(inputs, NCH):
    f = np.float32
    x = np.asarray(inputs["x"], f)
    B = x.shape[0]
    L = NCH * 128
    TOK = (NCH - 1) * 128
    TQ = TOK // 4
    T3 = min(512, TQ)
    NTB = TQ // T3
    NQB = (NCH - 1) // 4
    PS = min(TOK, 1024)
    CPP = PS // T3
    meta = np.asarray(inputs["meta"], f)
    w_in = np.asarray(inputs["w_in"], f)[0]
    conv_w = np.asarray(inputs["conv_w"], f)[0]
    conv_b = np.asarray(inputs["conv_b"], f)[0]
    rep = lambda v: np.ascontiguousarray(np.broadcast_to(np.asarray(v, f)[None, :], (128, len(v))))
    fm = lambda v: np.ascontiguousarray(np.asarray(v, f).reshape(-1, 128).T)
    ar = np.arange(128)
    identf = np.eye(128, dtype=f)
    umat = (ar[:, None] <= ar[None, :]).astype(f)
    mneg = np.where(ar[None, :] >= ar[:, None], 0.0, NEGBIG).astype(f)
    lamv = np.stack([rep(inputs[k][0]) for k in ("lam_q1", "lam_k1", "lam_q2", "lam_k2")], 1)
    shared = dict(
        wm=np.ascontiguousarray(w_in[:, MA_OFF:MA_OFF + 4096]),
        wa=np.asarray(inputs["w_br_attn"], f)[0], ws=np.asarray(inputs["w_br_ssm"], f)[0],
        wo=np.asarray(inputs["w_out"], f)[0],
        normg=fm(inputs["norm_g"][0]), ssmg=fm(inputs["ssm_norm_g"][0]),
        subln=rep(inputs["subln_g"][0]), finalg=rep(inputs["final_g"]), lamv=np.ascontiguousarray(lamv),
        identb=_bf(identf), identf=identf, tri=_bf(umat), umat=umat, mneg=mneg,
        kaug=_bf(np.ones((2, L), f)),
    )
    xps = []
    for b in range(B):
        xps.append(np.concatenate([np.zeros((PAD, D), f), meta, x[b]], 0))
    in_maps = []
    for c in range(8):
        b, g = c // 4, c % 4
        cols = np.concatenate([
            np.arange(Q_OFF + 2 * g * 128, Q_OFF + 2 * g * 128 + 256),
            np.arange(K_OFF + 2 * g * 128, K_OFF + 2 * g * 128 + 256),
            np.arange(V_OFF + 2 * g * 128, V_OFF + 2 * g * 128 + 256),
            np.arange(G_OFF + 2 * g * 128, G_OFF + 2 * g * 128 + 256),
            np.arange(Z_OFF + g * 512, Z_OFF + g * 512 + 512),
            np.arange(XBC_OFF + g * 512, XBC_OFF + g * 512 + 512),
            np.arange(XBC_OFF + 2048 + g * 128, XBC_OFF + 2048 + g * 128 + 128),
            np.arange(XBC_OFF + 2560 + g * 128, XBC_OFF + 2560 + g * 128 + 128),
            np.arange(DT_OFF + 8 * g, DT_OFF + 8 * g + 8)])
        ccols = cols[1536:2304] - XBC_OFF
        slopes = [2.0 ** (-(2 * g + hl + 1)) for hl in range(2)]
        btab = np.zeros((128, 2, 132), f)
        btab0 = np.zeros((128, 2, NQB), f)
        qaug = np.zeros((2, 2, L), f)
        tpos = np.arange(L)
        for hl, sl in enumerate(slopes):
            for i in range(132):
                btab[:, hl, i] = sl * (ar - 128.0 * (i - 3))
            for qb in range(NQB):
                btab0[:, hl, qb] = np.where(ar >= PAD, sl * (ar - 128.0 * (1 + 4 * qb)), NEGBIG)
            a_idx = ((tpos // 128 - 1) % 4).astype(f)
            qaug[hl, 0] = -sl * 128.0 * a_idx
            qaug[hl, 1] = -sl * (tpos % 128)
        idxx = (128 + g * TQ + np.arange(TQ)).reshape(TQ // 128, 128).T.astype(np.int32)
        idxs = np.zeros((128, 4 * 4 * NTB), np.int32)
        idxa = np.zeros((128, 4 * 2 * NTB), np.int32)
        for rk in range(4):
            for tb in range(NTB):
                t3 = g * NTB + tb
                pi, wi = t3 // CPP, t3 % CPP
                for ft in range(4):
                    idxs[:, (rk * 4 + ft) * NTB + tb] = (pi * 2048 + rk * 512 + ft * 128 + ar) * CPP + wi
                for ft in range(2):
                    idxa[:, (rk * 2 + ft) * NTB + tb] = (pi * 1024 + rk * 256 + ft * 128 + ar) * CPP + wi
        m = dict(shared)
        m.update(
            xp=xps[b], w1=np.ascontiguousarray(w_in[:, cols]),
            convw=np.ascontiguousarray(conv_w[:, ccols].T.reshape(6, 128, 4).transpose(1, 0, 2)),
            convb=np.ascontiguousarray(conv_b[ccols].reshape(6, 128).T),
            dtb=rep(inputs["dt_bias"][0][8 * g:8 * g + 8]), alog=rep(inputs["a_log"][0][8 * g:8 * g + 8]),
            dskip=rep(inputs["d_skip"][0][8 * g:8 * g + 8]),
            btab=btab, btab0=btab0, qaug=_bf(qaug), idxx=np.ascontiguousarray(idxx), idxs=idxs, idxa=idxa)
        in_maps.append(m)
    return in_maps


_NC_CACHE = {}


def kernel(**inputs):
    x = np.asarray(inputs["x"])
    B, S, _ = x.shape
    assert B == 2 and S % 128 == 0
    NCH = S // 128 + 1
    if NCH not in _NC_CACHE:
        import os
        _NC_CACHE[NCH] = build_program(NCH, int(os.environ.get('KSTOP', '99')))
    nc = _NC_CACHE[NCH]
    in_maps = make_in_maps(inputs, NCH)
    res = run_bass_kernel_spmd(nc, in_maps, core_ids=list(range(8)))
    TQ = S // 4
    outp = np.zeros((B, S, D), np.float32)
    for c in range(8):
        b, g = c // 4, c % 4
        outp[b, g * TQ:(g + 1) * TQ] = np.asarray(res.results[c]["out"], np.float32)
    return outp
```
